# Optimizing a Trainium2 kernel written in Bass

```python
import jax, jax.numpy as jnp
from jax import lax
import numpy as np

D_MODEL = 1024
BATCH = 4
SEQ = 4096
DEPTH = 2

EXPAND = 2
D_INNER = EXPAND * D_MODEL
BLOCK = 128
A_WIDTH = D_INNER // 2
A_GROUPS = 8
A_GROUP_DIM = A_WIDTH // A_GROUPS
B_HEADS = 8
B_HEAD_DIM = (D_INNER - A_WIDTH) // B_HEADS
B_WIDTH = B_HEADS * B_HEAD_DIM
C_WIDTH = D_INNER // 2
POOL_WINDOWS = (2, 4, 8, 16)
C_GROUPS = len(POOL_WINDOWS)
C_GROUP_DIM = C_WIDTH // C_GROUPS
D_HEADS = 8
D_HEAD_DIM = (D_INNER - C_WIDTH) // D_HEADS
D_WIDTH = D_HEADS * D_HEAD_DIM
ROPE_BASE = 10000.0
EPS = 1e-6

EVEN_SPLITS = (A_WIDTH, 2 * A_WIDTH, 2 * A_WIDTH + B_WIDTH, 2 * A_WIDTH + 2 * B_WIDTH, 2 * A_WIDTH + 3 * B_WIDTH)
ODD_SPLITS = (C_WIDTH, C_WIDTH + D_WIDTH, C_WIDTH + 2 * D_WIDTH, C_WIDTH + 3 * D_WIDTH)
EVEN_IN = EVEN_SPLITS[-1] + D_INNER
ODD_IN = ODD_SPLITS[-1] + D_INNER

kernel_name = "hybrid_gmlp_stickbreak_pool_retention_adaln"


def rms_norm(x, g):
    xf = x.astype(jnp.float32)
    y = xf * lax.rsqrt(jnp.mean(xf * xf, axis=-1, keepdims=True) + EPS)
    return (y * g.astype(jnp.float32)).astype(x.dtype)


def ada_modulation(c, w_mod, b_mod):
    m = jax.nn.silu(c) @ w_mod + b_mod
    shift, scale, gate = jnp.split(m, 3, axis=-1)
    return shift[:, None], scale[:, None], gate[:, None]


def to_heads(t, h, dh):
    b, s, _ = t.shape
    return t.reshape(b, s, h, dh).transpose(0, 2, 1, 3)


def from_heads(t):
    b, h, s, dh = t.shape
    return t.transpose(0, 2, 1, 3).reshape(b, s, h * dh)


def rotary(t, positions):
    half = t.shape[-1] // 2
    inv_freq = ROPE_BASE ** (-jnp.arange(half, dtype=jnp.float32) / half)
    ang = positions.astype(jnp.float32)[:, None, :, None] * inv_freq
    cos, sin = jnp.cos(ang), jnp.sin(ang)
    t1, t2 = t[..., :half].astype(jnp.float32), t[..., half:].astype(jnp.float32)
    out = jnp.concatenate([t1 * cos - t2 * sin, t1 * sin + t2 * cos], axis=-1)
    return out.astype(t.dtype)


def chunked_spatial_gating(u, v, v_norm_g, w_s, b_s):
    b, s, _ = u.shape
    nc = s // BLOCK
    vg = rms_norm(v.reshape(b, nc, BLOCK, A_GROUPS, A_GROUP_DIM), v_norm_g)
    causal = jnp.tril(jnp.ones((BLOCK, BLOCK), dtype=bool))
    w = jnp.where(causal[None], w_s, 0.0)
    mixed = jnp.einsum('gts,bnsgc->bntgc', w, vg) + b_s.T[:, :, None]
    return u * mixed.reshape(b, s, A_WIDTH)


def stick_breaking_attention(q, k, v):
    b, h, s, dh = q.shape
    nb = s // BLOCK
    scale = dh ** -0.5
    key_pos = jnp.arange(s)
    q_blocks = q.reshape(b, h, nb, BLOCK, dh).transpose(2, 0, 1, 3, 4)

    def one_block(args):
        q_blk, start = args
        z = jnp.einsum('bhtd,bhsd->bhts', q_blk, k).astype(jnp.float32) * scale
        q_pos = start + jnp.arange(BLOCK)
        before = key_pos[None, :] < q_pos[:, None]
        log_1m = jnp.where(before, jax.nn.log_sigmoid(-z), 0.0)
        between = lax.cumsum(log_1m, axis=3, reverse=True) - log_1m
        w = jnp.where(before, jnp.exp(jax.nn.log_sigmoid(z) + between), 0.0)
        return jnp.einsum('bhts,bhsd->bhtd', w.astype(v.dtype), v)

    starts = jnp.arange(nb, dtype=jnp.int32) * BLOCK
    out = lax.map(one_block, (q_blocks, starts))
    return out.transpose(1, 2, 0, 3, 4).reshape(b, h, s, dh)


def multiscale_pool(x, w_group, scale):
    b, s, _ = x.shape
    xg = x.reshape(b, s, C_GROUPS, C_GROUP_DIM)
    cs = jnp.cumsum(xg.astype(jnp.float32), axis=1)
    t = jnp.arange(s)
    pooled = []
    for gi, win in enumerate(POOL_WINDOWS):
        cg = cs[:, :, gi]
        lagged = jnp.pad(cg, ((0, 0), (win, 0), (0, 0)))[:, :s]
        count = jnp.minimum(t + 1, win).astype(jnp.float32)[None, :, None]
        pooled.append((cg - lagged) / count)
    pooled = (jnp.stack(pooled, axis=2) - xg.astype(jnp.float32)).astype(x.dtype)
    mixed = jnp.einsum('bsgc,gce->bsge', pooled, w_group)
    return mixed.reshape(b, s, C_WIDTH) * scale


def retention_chunkwise(q, k, v):
    b, h, s, dh = q.shape
    nc = s // BLOCK
    f32 = jnp.float32
    log_gamma = jnp.log1p(-jnp.exp2(-5.0 - jnp.arange(h, dtype=f32)))
    idx = jnp.arange(BLOCK, dtype=f32)
    diff = idx[:, None] - idx[None, :]
    intra_decay = jnp.where(diff >= 0, jnp.exp(log_gamma[:, None, None] * jnp.maximum(diff, 0.0)), 0.0)
    q_decay = jnp.exp(log_gamma[:, None] * (idx + 1.0))
    k_decay = jnp.exp(log_gamma[:, None] * (BLOCK - 1.0 - idx))
    chunk_decay = jnp.exp(log_gamma * BLOCK)
    qc = q.astype(f32).reshape(b, h, nc, BLOCK, dh)
    kc = k.astype(f32).reshape(b, h, nc, BLOCK, dh) * (dh ** -0.5)
    vc = v.astype(f32).reshape(b, h, nc, BLOCK, dh)
    scores = jnp.einsum('bhntd,bhnsd->bhnts', qc, kc) * intra_decay[None, :, None]
    intra = jnp.einsum('bhnts,bhnse->bhnte', scores, vc)
    kv = jnp.einsum('bhnsd,bhnse->nbhde', kc * k_decay[None, :, None, :, None], vc)

    def step(state, kv_n):
        return state * chunk_decay[None, :, None, None] + kv_n, state

    _, prev = lax.scan(step, jnp.zeros((b, h, dh, dh), f32), kv)
    cross = jnp.einsum('bhntd,nbhde->bhnte', qc * q_decay[None, :, None, :, None], prev)
    out = (intra + cross).reshape(b, h, s, dh)
    out = out * lax.rsqrt(jnp.mean(out * out, axis=-1, keepdims=True) + EPS)
    return out.astype(q.dtype)


def even_layer(x, c, norm_g, w_mod, b_mod, w_in, a_vnorm_g, a_ws, a_bs, b_qnorm_g, b_knorm_g, w_out):
    shift, scale, gate = ada_modulation(c, w_mod, b_mod)
    hdn = rms_norm(x, norm_g) * (1.0 + scale) + shift
    p = hdn @ w_in
    u, v, q, k, val, z = jnp.split(p, list(EVEN_SPLITS), axis=-1)
    a_out = chunked_spatial_gating(u, v, a_vnorm_g, a_ws, a_bs)
    qh = rms_norm(to_heads(q, B_HEADS, B_HEAD_DIM), b_qnorm_g)
    kh = rms_norm(to_heads(k, B_HEADS, B_HEAD_DIM), b_knorm_g)
    b_out = from_heads(stick_breaking_attention(qh, kh, to_heads(val, B_HEADS, B_HEAD_DIM)))
    y = jnp.concatenate([a_out, b_out], axis=-1) * jax.nn.silu(z)
    return x + gate * (y @ w_out)


def odd_layer(x, c, positions, norm_g, w_mod, b_mod, w_in, c_w, c_scale, d_qnorm_g, d_knorm_g, w_out):
    shift, scale, gate = ada_modulation(c, w_mod, b_mod)
    hdn = rms_norm(x, norm_g) * (1.0 + scale) + shift
    p = hdn @ w_in
    pc, q, k, val, z = jnp.split(p, list(ODD_SPLITS), axis=-1)
    c_out = multiscale_pool(pc, c_w, c_scale)
    qh = rotary(rms_norm(to_heads(q, D_HEADS, D_HEAD_DIM), d_qnorm_g), positions)
    kh = rotary(rms_norm(to_heads(k, D_HEADS, D_HEAD_DIM), d_knorm_g), positions)
    d_out = from_heads(retention_chunkwise(qh, kh, to_heads(val, D_HEADS, D_HEAD_DIM)))
    y = jnp.concatenate([c_out, d_out], axis=-1) * jax.nn.silu(z)
    return x + gate * (y @ w_out)


def setup_inputs(seed: int = 0) -> dict:
    key = jax.random.key(seed)
    ks = iter(jax.random.split(key, 32))
    f32 = jnp.float32
    ne = (DEPTH + 1) // 2
    no = DEPTH // 2

    def nrm(shape, s):
        return jax.random.normal(next(ks), shape, f32) * s

    x = nrm((BATCH, SEQ, D_MODEL), 1.0)
    c = nrm((BATCH, D_MODEL), 1.0)
    offsets = jax.random.randint(next(ks), (BATCH, 1), 0, 1024, dtype=jnp.int32)
    positions = jnp.arange(SEQ, dtype=jnp.int32)[None, :] + offsets
    return {
        "x": x, "c": c, "positions": positions,
        "even_norm_g": 1.0 + nrm((ne, D_MODEL), 0.02),
        "even_w_mod": nrm((ne, D_MODEL, 3 * D_MODEL), 0.5 * D_MODEL ** -0.5),
        "even_b_mod": nrm((ne, 3 * D_MODEL), 0.02),
        "even_w_in": nrm((ne, D_MODEL, EVEN_IN), D_MODEL ** -0.5),
        "even_a_vnorm_g": 1.0 + nrm((ne, A_GROUPS, A_GROUP_DIM), 0.02),
        "even_a_ws": nrm((ne, A_GROUPS, BLOCK, BLOCK), BLOCK ** -0.5),
        "even_a_bs": 1.0 + nrm((ne, A_GROUPS, BLOCK), 0.1),
        "even_b_qnorm_g": 1.0 + nrm((ne, B_HEAD_DIM), 0.02),
        "even_b_knorm_g": 1.0 + nrm((ne, B_HEAD_DIM), 0.02),
        "even_w_out": nrm((ne, D_INNER, D_MODEL), D_INNER ** -0.5),
        "odd_norm_g": 1.0 + nrm((no, D_MODEL), 0.02),
        "odd_w_mod": nrm((no, D_MODEL, 3 * D_MODEL), 0.5 * D_MODEL ** -0.5),
        "odd_b_mod": nrm((no, 3 * D_MODEL), 0.02),
        "odd_w_in": nrm((no, D_MODEL, ODD_IN), D_MODEL ** -0.5),
        "odd_c_w": nrm((no, C_GROUPS, C_GROUP_DIM, C_GROUP_DIM), C_GROUP_DIM ** -0.5),
        "odd_c_scale": 1.0 + nrm((no, C_WIDTH), 0.1),
        "odd_d_qnorm_g": 1.0 + nrm((no, D_HEAD_DIM), 0.02),
        "odd_d_knorm_g": 1.0 + nrm((no, D_HEAD_DIM), 0.02),
        "odd_w_out": nrm((no, D_INNER, D_MODEL), D_INNER ** -0.5),
    }


def reference(x, c, positions,
              even_norm_g, even_w_mod, even_b_mod, even_w_in, even_a_vnorm_g, even_a_ws, even_a_bs,
              even_b_qnorm_g, even_b_knorm_g, even_w_out,
              odd_norm_g, odd_w_mod, odd_b_mod, odd_w_in, odd_c_w, odd_c_scale,
              odd_d_qnorm_g, odd_d_knorm_g, odd_w_out):
    for layer in range(DEPTH):
        i = layer // 2
        if layer % 2 == 0:
            x = even_layer(x, c, even_norm_g[i], even_w_mod[i], even_b_mod[i], even_w_in[i],
                           even_a_vnorm_g[i], even_a_ws[i], even_a_bs[i],
                           even_b_qnorm_g[i], even_b_knorm_g[i], even_w_out[i])
        else:
            x = odd_layer(x, c, positions, odd_norm_g[i], odd_w_mod[i], odd_b_mod[i], odd_w_in[i],
                          odd_c_w[i], odd_c_scale[i], odd_d_qnorm_g[i], odd_d_knorm_g[i], odd_w_out[i])
    return x
```

```python
import numpy as np
import concourse.bass as bass
import concourse.mybir as mybir
from concourse.bass_utils import run_bass_kernel_spmd

F32 = mybir.dt.float32
BF16 = mybir.dt.bfloat16
I32 = mybir.dt.int32
AF = mybir.ActivationFunctionType
ALU = mybir.AluOpType
AX = mybir.AxisListType

S = 4096
D = 1024
NB = S // 128
NT = S // 512
EPS = 1e-6
SAME_ENGINE_SYNC = True
DBG = {"nt": NT, "stage": 9}


class Buf:
    __slots__ = ("name", "last_writer", "readers")

    def __init__(self, name):
        self.name = name
        self.last_writer = None
        self.readers = []


class Instr:
    __slots__ = ("eng", "fn", "deps", "is_dma", "sem", "semval", "signal", "idx")

    def __init__(self, eng, fn, deps, is_dma):
        self.eng = eng
        self.fn = fn
        self.deps = deps
        self.is_dma = is_dma
        self.sem = None
        self.semval = None
        self.signal = False


class Prog:
    ENGS = ("pe", "act", "dve", "pool", "sp")

    def __init__(self, nc):
        self.nc = nc
        self.streams = {e: [] for e in self.ENGS}
        self.dma_sems = {}
        self.order = []

    def emit(self, eng, fn, reads=(), writes=(), deps=(), dma_key=None):
        d = []
        for b in reads:
            if b.last_writer is not None:
                d.append(b.last_writer)
        for b in writes:
            if b.last_writer is not None:
                d.append(b.last_writer)
            d.extend(b.readers)
        d.extend(x for x in deps if x is not None)
        ins = Instr(eng, fn, d, dma_key is not None)
        if dma_key == "const":
            self.nconst = getattr(self, "nconst", 0) + 1
            dma_key = f"c{self.nconst}"
        if dma_key is not None:
            ins.sem = dma_key
        for b in reads:
            b.readers.append(ins)
        for b in writes:
            b.last_writer = ins
            b.readers = []
        self.streams[eng].append(ins)
        self.order.append(ins)
        return ins

    def finalize(self, block_ctx_fn):
        nc = self.nc
        for ins in self.order:
            for dpd in ins.deps:
                if dpd.is_dma:
                    continue
                if dpd.eng == ins.eng and (dpd.eng == "pe" or not SAME_ENGINE_SYNC):
                    continue
                dpd.signal = True
        counts = {e: 0 for e in self.ENGS}
        dma_counts = {}
        for e in self.ENGS:
            for ins in self.streams[e]:
                if ins.is_dma:
                    k = ins.sem
                    dma_counts[k] = dma_counts.get(k, 0) + 16
                    ins.semval = dma_counts[k]
                elif ins.signal:
                    counts[e] += 1
                    ins.semval = counts[e]
        return counts, dma_counts

    def run(self, final_waits=()):
        nc = self.nc
        counts, dma_counts = self.finalize(None)
        import contextlib
        with contextlib.ExitStack() as st:
            esem = {e: st.enter_context(nc.semaphore("s_" + e)) for e in self.ENGS}
            dsem = {k: st.enter_context(nc.semaphore("d_" + str(k))) for k in dma_counts}
            block = st.enter_context(nc.Block())

            def make(e):
                stream = self.streams[e]

                def body(eng):
                    seen = {}
                    for ins in stream:
                        for dpd in ins.deps:
                            if dpd.is_dma:
                                key = ("d", dpd.sem)
                                sem = dsem[dpd.sem]
                            else:
                                if dpd.eng == e and (e == "pe" or not SAME_ENGINE_SYNC):
                                    continue
                                key = ("e", dpd.eng)
                                sem = esem[dpd.eng]
                            if seen.get(key, 0) >= dpd.semval:
                                continue
                            eng.wait_ge(sem, dpd.semval)
                            seen[key] = dpd.semval
                        r = ins.fn(eng)
                        if ins.is_dma:
                            r.then_inc(dsem[ins.sem], 16)
                        elif ins.signal:
                            r.then_inc(esem[e], 1)
                    if e == "sp":
                        for k in final_waits:
                            eng.wait_ge(dsem[k], dma_counts[k])
                return body

            block.tensor(make("pe"))
            block.scalar(make("act"))
            block.vector(make("dve"))
            block.gpsimd(make("pool"))
            block.sync(make("sp"))


def barrier_list(P):
    out = []
    lastdma = {}
    for e in P.ENGS:
        lastc = None
        for ins in P.streams[e]:
            if ins.is_dma:
                lastdma[ins.sem] = ins
            else:
                lastc = ins
        if lastc is not None:
            out.append(lastc)
    out.extend(lastdma.values())
    return out


class Arena:
    def __init__(self, nc, stack, nbytes):
        self.t = stack.enter_context(nc.sbuf_tensor("arena", [128, nbytes // 2], BF16))
        self.nbytes = nbytes
        self.off = 0
        self.barrier = []
        self.peak = 0

    def alloc(self, name, shape, dtype):
        esz = 2 if dtype == BF16 else 4
        n = int(np.prod(shape[1:])) * esz
        n = (n + 63) // 64 * 64
        assert self.off + n <= self.nbytes, f"arena overflow at {name}: {self.off}+{n} > {self.nbytes}"
        ap = self.t[:, self.off // 2:(self.off + n) // 2]
        self.off += n
        self.peak = max(self.peak, self.off)
        if dtype != BF16:
            ap = ap.bitcast(dtype)
        ne = int(np.prod(shape[1:]))
        ap = ap[:, 0:ne]
        if len(shape) == 3:
            ap = ap.rearrange("p (a b) -> p a b", a=shape[1])
        elif len(shape) == 4:
            ap = ap.rearrange("p (a b c) -> p a b c", a=shape[1], b=shape[2])
        bf = Buf(name)
        bf.readers = list(self.barrier)
        return ap, bf

    def mark(self):
        return self.off

    def release(self, mark, P):
        self.off = mark
        self.barrier = barrier_list(P)


LAYER_PARAMS = {
    0: [("c", [1, D], F32), ("norm_g", [1, D], F32), ("w_mod", [D, 3 * D], F32), ("b_mod", [1, 3 * D], F32), ("w_out", [D, D], F32),
        ("w_p1", [D, 2048], F32), ("w_p2", [D, 1536], F32), ("vnorm_g", [1, 512], F32), ("a_ws", [4, 128, 128], F32), ("a_bs", [1, 512], F32),
        ("qnorm_g", [1, 128], F32), ("knorm_g", [1, 128], F32)],
    1: [("c", [1, D], F32), ("norm_g", [1, D], F32), ("w_mod", [D, 3 * D], F32), ("b_mod", [1, 3 * D], F32), ("w_out", [D, D], F32),
        ("w_p", [D, 3584], F32), ("cw", [4, 256, 128], F32), ("cscale", [1, 512], F32),
        ("qnorm_g", [1, 128], F32), ("knorm_g", [1, 128], F32), ("pos", [1, S], I32), ("lgam", [1, 4], F32), ("invf", [1, 64], F32)],
}
PAIR_GROUPS = [[0, 1], [2, 3], [4, 5], [6, 7]]


def build_program(mode):
    import contextlib
    nc = bass.Bass("TRN2", target_bir_lowering=False)
    P = Prog(nc)
    dt = nc.dram_tensor

    if mode == "fin":
        fa = dt("fa", [2048, D], F32, kind="ExternalInput").ap()
        fb = dt("fb", [2048, D], F32, kind="ExternalInput").ap()
        out = dt("out", [2048, D], F32, kind="ExternalOutput").ap()
        with contextlib.ExitStack() as st:
            bufs = []
            for i in range(2):
                ta = st.enter_context(nc.sbuf_tensor(f"ta{i}", [128, 4, D], F32))
                tb = st.enter_context(nc.sbuf_tensor(f"tb{i}", [128, 4, D], F32))
                bufs.append((ta, Buf(f"ta{i}"), tb, Buf(f"tb{i}")))
            for it in range(4):
                ta, ba, tb, bb = bufs[it % 2]
                rows = slice(it * 512, (it + 1) * 512)
                P.emit("sp", lambda e, ta=ta, rows=rows: e.dma_start(out=ta[:], in_=fa[rows, :].rearrange("(k p) f -> p k f", p=128)),
                       writes=[ba], dma_key=f"la{it % 2}")
                P.emit("sp", lambda e, tb=tb, rows=rows: e.dma_start(out=tb[:], in_=fb[rows, :].rearrange("(k p) f -> p k f", p=128)),
                       writes=[bb], dma_key=f"lb{it % 2}")
                for k in range(4):
                    eng = "dve" if k % 2 == 0 else "pool"
                    P.emit(eng, lambda e, ta=ta, tb=tb, k=k: e.tensor_tensor(out=ta[:, k, :], in0=ta[:, k, :], in1=tb[:, k, :], op=ALU.add),
                           reads=[bb], writes=[ba])
                P.emit("sp", lambda e, ta=ta, rows=rows: e.dma_start(out=out[rows, :].rearrange("(k p) f -> p k f", p=128), in_=ta[:]),
                       reads=[ba], dma_key=f"st{it % 2}")
            P.run(final_waits=["st0", "st1"])
        return nc

    layers = {"l0": [0], "l1": [1], "fused": [0, 1]}[mode]
    DR = {}
    for L in layers:
        for name, shape, dty in LAYER_PARAMS[L]:
            DR[(L, name)] = dt(f"p{L}_{name}", shape, dty, kind="ExternalInput").ap()
    b_dram = {}
    if mode == "l0":
        x0 = dt("x", [S, D], F32, kind="ExternalInput").ap()
        f0 = dt("fout", [S, D], F32, kind="ExternalOutput").ap()
        srcs = {0: (x0, None)}
        dsts = {0: f0}
    elif mode == "l1":
        xa = dt("xa", [S, D], F32, kind="ExternalInput").ap()
        xb_ = dt("xb", [S, D], F32, kind="ExternalInput").ap()
        f1 = dt("fout", [S, D], F32, kind="ExternalOutput").ap()
        srcs = {1: (xa, xb_)}
        dsts = {1: f1}
    else:
        x0 = dt("x", [S, D], F32, kind="ExternalInput").ap()
        f0_t = dt("f0", [S, D], F32)
        x1_t = dt("x1", [S, D], F32)
        f1_t = dt("f1", [S, D], F32)
        rs_t = dt("rs", [S // 2, D], F32)
        out = dt("out", [S // 2, D], F32, kind="ExternalOutput").ap()
        srcs = {0: (x0, None), 1: (x1_t.ap(), None)}
        dsts = {0: f0_t.ap(), 1: f1_t.ap()}
    b_f = {0: Buf("f0"), 1: Buf("f1")}
    b_x1 = Buf("x1d")

    with contextlib.ExitStack() as st:
        def sb(name, shape, dty):
            t = st.enter_context(nc.sbuf_tensor(name, shape, dty))
            return t, Buf(name)

        onesf, b_onesf = sb("onesf", [128, 128], F32)
        ident, b_ident = sb("ident", [128, 128], BF16)
        identf, b_identf = sb("identf", [128, 128], F32)
        negtri, b_negtri = sb("negtri", [128, 128], BF16)
        negones, b_negones = sb("negones", [128, 128], BF16)
        zerosb, b_zerosb = sb("zerosb", [128, 128], BF16)
        mask01, b_mask01 = sb("mask01", [128, 128], BF16)
        maskle, b_maskle = sb("maskle", [128, 128], F32)
        tril01, b_tril01 = sb("tril01", [128, 128], F32)
        ones_row, b_ones_row = sb("ones_row", [1, 128], F32)
        P.emit("pool", lambda e: e.memset(onesf[:], 1.0), writes=[b_onesf])
        P.emit("pool", lambda e: e.memset(ones_row[:], 1.0), writes=[b_ones_row])
        P.emit("pool", lambda e: e.memset(negones[:], -1.0), writes=[b_negones])
        P.emit("pool", lambda e: e.memset(zerosb[:], 0.0), writes=[b_zerosb])

        def asel(out_t, b_out, in_t, b_in, pattern, op, cm):
            P.emit("pool", lambda e: e.affine_select(out=out_t[:], in_=in_t[:], pattern=pattern, compare_op=op, fill=0.0, base=0, channel_multiplier=cm),
                   reads=[b_in], writes=[b_out])
        asel(ident, b_ident, onesf, b_onesf, [[-1, 128]], ALU.is_equal, 1)
        asel(identf, b_identf, onesf, b_onesf, [[-1, 128]], ALU.is_equal, 1)
        asel(negtri, b_negtri, negones, b_negones, [[-1, 128]], ALU.is_ge, 1)
        asel(mask01, b_mask01, onesf, b_onesf, [[1, 128]], ALU.is_gt, -1)
        asel(maskle, b_maskle, onesf, b_onesf, [[1, 128]], ALU.is_ge, -1)
        asel(tril01, b_tril01, onesf, b_onesf, [[-1, 128]], ALU.is_ge, 1)

        PS = []
        for i in range(8):
            t = st.enter_context(nc.psum_tensor(f"ps{i}", [128, 512], F32))
            PS.append((t, Buf(f"ps{i}")))

        arena = Arena(nc, st, 200 * 1024)
        C = dict(nc=nc, P=P, arena=arena, PS=PS, DR=DR, mode=mode,
                 onesf=(onesf, b_onesf), ident=(ident, b_ident), identf=(identf, b_identf), negtri=(negtri, b_negtri),
                 negones=(negones, b_negones), zerosb=(zerosb, b_zerosb), mask01=(mask01, b_mask01), maskle=(maskle, b_maskle),
                 tril01=(tril01, b_tril01), ones_row=(ones_row, b_ones_row))

        for L in layers:
            m0 = arena.mark()
            xsrc, xsrc2 = srcs[L]
            C.update(L=L, xsrc=xsrc, xsrc2=xsrc2, fout=dsts[L], b_fout=b_f[L], b_xsrc=(b_x1 if (mode == "fused" and L == 1) else None))
            layer_common(C)
            if L == 0:
                build_layer0(C)
            else:
                build_layer1(C)
            arena.release(m0, P)
            if mode == "fused":
                if L == 0:
                    P.emit("pool", lambda e: e.collective_compute("AllReduce", ALU.add, replica_groups=PAIR_GROUPS,
                                                                  ins=[f0_t.ap().opt()], outs=[x1_t.ap().opt()]),
                           reads=[b_f[0]], writes=[b_x1], dma_key="cc0")
                else:
                    b_rs = Buf("rs")
                    P.emit("pool", lambda e: e.collective_compute("ReduceScatter", ALU.add, replica_groups=PAIR_GROUPS,
                                                                  ins=[f1_t.ap().opt()], outs=[rs_t.ap().opt()]),
                           reads=[b_f[1]], writes=[b_rs], dma_key="cc1")
                    P.emit("sp", lambda e: e.dma_start(out=out[:, :], in_=rs_t.ap()[:, :]), reads=[b_rs], dma_key="fin")
        print("arena peak bytes", arena.peak)
        if mode == "fused":
            P.run(final_waits=["fin"])
        else:
            P.run(final_waits=["fo0", "fo1"])
    return nc


def layer_common(C):
    nc, P, arena, PS, DR, L = C["nc"], C["P"], C["arena"], C["PS"], C["DR"], C["L"]
    onesf, b_onesf = C["onesf"]
    identf, b_identf = C["identf"]
    ident, b_ident = C["ident"]
    c_in, norm_g, w_mod, b_mod = DR[(L, "c")], DR[(L, "norm_g")], DR[(L, "w_mod")], DR[(L, "b_mod")]

    cols, b_cols = arena.alloc("cols", [128, 96], F32)
    CC0, NGC0, BMC0, SC0, MC0, GP0 = 0, 8, 16, 40, 48, 72
    SHC0, SLC0, GTC0 = MC0, MC0 + 8, MC0 + 16

    def col_load(dst_ap, src, b_dst):
        P.emit("sp", lambda e: e.dma_start(out=dst_ap.unsqueeze(2), in_=src.rearrange("o (c p u) -> p (o c) u", p=128, u=1),
                                           allow_slow_non_contiguous=True), writes=[b_dst], dma_key="const")
    C["col_load"] = col_load
    col_load(cols[:, CC0:CC0 + 8], c_in, b_cols)
    col_load(cols[:, NGC0:NGC0 + 8], norm_g, b_cols)
    col_load(cols[:, BMC0:BMC0 + 24], b_mod, b_cols)
    P.emit("act", lambda e: e.activation(out=cols[:, SC0:SC0 + 8], in_=cols[:, CC0:CC0 + 8], func=AF.Silu), reads=[b_cols], writes=[b_cols])

    stg = [arena.alloc(f"stg{i}", [128, 8, 128], F32) for i in range(2)]
    stg_i = [0]

    def stage_load(src_ap_fn):
        i = stg_i[0] % 2
        stg_i[0] += 1
        t, b = stg[i]
        P.emit("sp", lambda e: e.dma_start(out=t, in_=src_ap_fn()), writes=[b], dma_key=f"stg{i}")
        return t, b

    w_mod_v = w_mod.rearrange("(c p) n -> p c n", p=128)
    pt0, pb0 = PS[0]
    for cb in range(24):
        t, b = stage_load(lambda cb=cb: w_mod_v[:, :, cb * 128:(cb + 1) * 128])
        for fc in range(8):
            P.emit("pe", lambda e, t=t, fc=fc, cb=cb: e.matmul(pt0[:, cb:cb + 1], lhsT=t[:, fc, :], rhs=cols[:, SC0 + fc:SC0 + fc + 1],
                                                               start=(fc == 0), stop=(fc == 7)),
                   reads=[b, b_cols], writes=[pb0])
    P.emit("dve", lambda e: e.tensor_tensor(out=cols[:, MC0:MC0 + 24], in0=pt0[:, 0:24], in1=cols[:, BMC0:BMC0 + 24], op=ALU.add),
           reads=[pb0, b_cols], writes=[b_cols])
    P.emit("dve", lambda e: e.scalar_tensor_tensor(out=cols[:, GP0:GP0 + 8], in0=cols[:, SLC0:SLC0 + 8], scalar=1.0, in1=cols[:, NGC0:NGC0 + 8],
                                                   op0=ALU.add, op1=ALU.mult), reads=[b_cols], writes=[b_cols])
    gate_bc, b_gate = arena.alloc("gate_bc", [128, D], F32)
    gl, b_gl = arena.alloc("gl", [128, 128], F32)
    for fc in range(8):
        pt, pb = PS[1 + (fc // 4)]
        P.emit("dve", lambda e, fc=fc: e.tensor_copy(out=gl, in_=cols[:, GTC0 + fc:GTC0 + fc + 1].to_broadcast([128, 128])),
               reads=[b_cols], writes=[b_gl])
        P.emit("pe", lambda e, pt=pt, fc=fc: e.matmul(pt[:, (fc % 4) * 128:(fc % 4 + 1) * 128], lhsT=gl, rhs=identf[:], start=True, stop=True),
               reads=[b_gl, b_identf], writes=[pb])
        if fc % 4 == 3:
            P.emit("dve", lambda e, pt=pt, fc=fc: e.tensor_copy(out=gate_bc[:, (fc // 4) * 512:(fc // 4 + 1) * 512], in_=pt[:, :]), reads=[pb], writes=[b_gate])

    cast_rr = [0]

    def load_weight_bf16(dst, b_dst, src, ncols):
        src_v = src.rearrange("(c p) n -> p c n", p=128)
        for cb in range(ncols // 128):
            t, b = stage_load(lambda cb=cb: src_v[:, :, cb * 128:(cb + 1) * 128])
            eng = "pool" if cast_rr[0] % 2 == 0 else "dve"
            cast_rr[0] += 1
            P.emit(eng, lambda e, t=t, cb=cb: e.tensor_copy(out=dst[:, :, cb * 128:(cb + 1) * 128], in_=t), reads=[b], writes=[b_dst])

    xs = [arena.alloc(f"xs{i}", [128, D], F32) for i in range(2)]
    xs2 = arena.alloc("xs2", [128, D], F32) if C["xsrc2"] is not None else None
    xn = [arena.alloc(f"xn{i}", [128, D], BF16) for i in range(2)]
    stat, b_stat = arena.alloc("stat", [128, 16], F32)
    hdnT, b_hdnT = arena.alloc("hdnT", [128, 8, 512], BF16)
    sq, b_sq = arena.alloc("sq", [128, 512], F32)
    tmpf, b_tmpf = arena.alloc("tmpf", [128, 512], F32)
    tmp8, b_tmp8 = arena.alloc("tmp8", [128, 8, 128], F32)
    xsrc, xsrc2, b_xsrc = C["xsrc"], C["xsrc2"], C["b_xsrc"]
    x_reads = [b_xsrc] if b_xsrc is not None else []

    def load_x(tb, xt, xb, key):
        P.emit("sp", lambda e: e.dma_start(out=xt, in_=xsrc[tb * 128:(tb + 1) * 128, :]), reads=x_reads, writes=[xb], dma_key=key)
        if xsrc2 is not None:
            t2, b2 = xs2
            P.emit("sp", lambda e: e.dma_start(out=t2, in_=xsrc2[tb * 128:(tb + 1) * 128, :]), writes=[b2], dma_key="xs2")
            P.emit("pool", lambda e: e.tensor_tensor(out=xt, in0=xt, in1=t2, op=ALU.add), reads=[b2], writes=[xb])

    x_ctr = [0]

    def compute_hdnT(g):
        for blk in range(4):
            tb = 4 * g + blk
            i = x_ctr[0] % 2
            x_ctr[0] += 1
            xt, xb = xs[i]
            xnt, xnb = xn[i]
            load_x(tb, xt, xb, f"xs{i}")
            P.emit("act", lambda e, xt=xt: e.activation(out=tmp8.rearrange("p c t -> p (c t)"), in_=xt, func=AF.Square, accum_out=stat[:, 0:1]),
                   reads=[xb], writes=[b_tmp8, b_stat])
            P.emit("act", lambda e: e.activation(out=stat[:, 1:2], in_=stat[:, 0:1], func=AF.Ln, scale=1.0 / D, bias=EPS),
                   reads=[b_stat], writes=[b_stat])
            P.emit("act", lambda e: e.activation(out=stat[:, 2:3], in_=stat[:, 1:2], func=AF.Exp, scale=-0.5),
                   reads=[b_stat], writes=[b_stat])
            P.emit("pool", lambda e, xt=xt, xnt=xnt: e.tensor_scalar(out=xnt, in0=xt, scalar1=stat[:, 2:3], scalar2=None, op0=ALU.mult),
                   reads=[xb, b_stat], writes=[xnb])
            pt, pb = PS[2]
            ptb = pt[:].bitcast(BF16)
            for fc in range(8):
                P.emit("pe", lambda e, fc=fc, xnt=xnt, ptb=ptb: e.transpose(ptb[:, fc * 128:(fc + 1) * 128], xnt[:, fc * 128:(fc + 1) * 128], ident[:]),
                       reads=[xnb, b_ident], writes=[pb])
            tp3 = ptb.rearrange("p (c t) -> p c t", c=8)
            P.emit("dve", lambda e, tp3=tp3: e.tensor_tensor(out=tmp8, in0=tp3, in1=cols[:, GP0:GP0 + 8].unsqueeze(2).to_broadcast([128, 8, 128]), op=ALU.mult),
                   reads=[pb, b_cols], writes=[b_tmp8])
            P.emit("dve", lambda e, blk=blk: e.tensor_tensor(out=hdnT[:, :, blk * 128:(blk + 1) * 128], in0=tmp8,
                                                              in1=cols[:, SHC0:SHC0 + 8].unsqueeze(2).to_broadcast([128, 8, 128]), op=ALU.add),
                   reads=[b_tmp8, b_cols], writes=[b_hdnT])

    def rms_rstd(n, out_col):
        P.emit("dve", lambda e: e.tensor_reduce(out=stat[:, 4:8], in_=sq.rearrange("p (h d) -> p h d", h=4), axis=AX.X, op=ALU.add),
               reads=[b_sq], writes=[b_stat])
        P.emit("act", lambda e: e.activation(out=stat[:, 8:12], in_=stat[:, 4:8], func=AF.Ln, scale=1.0 / n, bias=EPS),
               reads=[b_stat], writes=[b_stat])
        P.emit("act", lambda e: e.activation(out=stat[:, out_col:out_col + 4], in_=stat[:, 8:12], func=AF.Exp, scale=-0.5),
               reads=[b_stat], writes=[b_stat])

    def headnorm_psum(pt, pb, out_ap, out_b, out_eng="dve"):
        P.emit("act", lambda e: e.activation(out=sq, in_=pt[:, :], func=AF.Square), reads=[pb], writes=[b_sq])
        rms_rstd(128.0, 12)
        P.emit(out_eng, lambda e: e.tensor_tensor(out=out_ap, in0=pt[:, :].rearrange("p (h d) -> p h d", h=4),
                                                  in1=stat[:, 12:16].unsqueeze(2).to_broadcast([128, 4, 128]), op=ALU.mult),
               reads=[pb, b_stat], writes=[out_b])

    acc_rr = [0]

    def acc_bank():
        i = acc_rr[0] % 2
        acc_rr[0] += 1
        return PS[i]

    def proj_tok(W, b_W, col0, blk):
        pt, pb = acc_bank()
        for fc in range(8):
            P.emit("pe", lambda e, fc=fc: e.matmul(pt[:, :], lhsT=hdnT[:, fc, blk * 128:(blk + 1) * 128], rhs=W[:, fc, col0:col0 + 512],
                                                   start=(fc == 0), stop=(fc == 7)), reads=[b_hdnT, b_W], writes=[pb])
        return pt, pb

    def proj_feat(W, b_W, col0):
        pt, pb = acc_bank()
        for fc in range(8):
            P.emit("pe", lambda e, fc=fc: e.matmul(pt[:, :], lhsT=W[:, fc, col0:col0 + 128], rhs=hdnT[:, fc, :],
                                                   start=(fc == 0), stop=(fc == 7)), reads=[b_hdnT, b_W], writes=[pb])
        return pt, pb

    xo = None

    def out_proj(g, Wo, b_Wo, lhs_fn, xo):
        fout, b_fout = C["fout"], C["b_fout"]
        for blk in range(4):
            tb = 4 * g + blk
            i = C.setdefault("o_ctr", 0) % 2
            C["o_ctr"] += 1
            xt, xb = xo[i]
            load_x(tb, xt, xb, f"xo{i}")
            for nh in range(2):
                po, pob = acc_bank()
                for ic in range(8):
                    lt, lb = lhs_fn(ic, blk)
                    P.emit("pe", lambda e, lt=lt, ic=ic, po=po, nh=nh: e.matmul(po[:, :], lhsT=lt, rhs=Wo[:, ic, nh * 512:(nh + 1) * 512],
                                                                  start=(ic == 0), stop=(ic == 7)), reads=[lb, b_Wo], writes=[pob])
                P.emit("dve", lambda e, po=po, nh=nh: e.tensor_tensor(out=tmpf, in0=po[:, :], in1=gate_bc[:, nh * 512:(nh + 1) * 512], op=ALU.mult),
                       reads=[pob, b_gate], writes=[b_tmpf])
                P.emit("dve", lambda e, xt=xt, nh=nh: e.scalar_tensor_tensor(out=xt[:, nh * 512:(nh + 1) * 512], in0=xt[:, nh * 512:(nh + 1) * 512], scalar=0.5,
                                                                             in1=tmpf, op0=ALU.mult, op1=ALU.add), reads=[b_tmpf], writes=[xb])
            P.emit("sp", lambda e, xt=xt, tb=tb: e.dma_start(out=fout[tb * 128:(tb + 1) * 128, :], in_=xt), reads=[xb], writes=[b_fout], dma_key=f"fo{i}")

    C.update(cols=cols, b_cols=b_cols, gate_bc=gate_bc, b_gate=b_gate, load_weight_bf16=load_weight_bf16, stage_load=stage_load,
             stat=stat, b_stat=b_stat, hdnT=hdnT, b_hdnT=b_hdnT, sq=sq, b_sq=b_sq, tmpf=tmpf, b_tmpf=b_tmpf, tmp8=tmp8, b_tmp8=b_tmp8,
             compute_hdnT=compute_hdnT, rms_rstd=rms_rstd, headnorm_psum=headnorm_psum, acc_bank=acc_bank, proj_tok=proj_tok,
             proj_feat=proj_feat, out_proj=out_proj, load_x=load_x)


def build_layer0(C):
    nc, P, arena, PS, DR, L = C["nc"], C["P"], C["arena"], C["PS"], C["DR"], C["L"]
    ident, b_ident = C["ident"]
    negtri, b_negtri = C["negtri"]
    negones, b_negones = C["negones"]
    zerosb, b_zerosb = C["zerosb"]
    mask01, b_mask01 = C["mask01"]
    tril01, b_tril01 = C["tril01"]
    ones_row, b_ones_row = C["ones_row"]
    stat, b_stat, sq, b_sq, tmpf, b_tmpf = C["stat"], C["b_stat"], C["sq"], C["b_sq"], C["tmpf"], C["b_tmpf"]
    hdnT, b_hdnT = C["hdnT"], C["b_hdnT"]
    load_weight_bf16, compute_hdnT, headnorm_psum, acc_bank = C["load_weight_bf16"], C["compute_hdnT"], C["headnorm_psum"], C["acc_bank"]
    proj_tok, proj_feat, out_proj, rms_rstd = C["proj_tok"], C["proj_feat"], C["out_proj"], C["rms_rstd"]
    w_p1, w_p2, w_out = DR[(L, "w_p1")], DR[(L, "w_p2")], DR[(L, "w_out")]
    vnorm_g, a_ws, a_bs, qnorm_g, knorm_g = DR[(L, "vnorm_g")], DR[(L, "a_ws")], DR[(L, "a_bs")], DR[(L, "qnorm_g")], DR[(L, "knorm_g")]

    W1, b_W1 = arena.alloc("W1", [128, 8, 2048], BF16)
    YB, b_YB = arena.alloc("YB", [128, 4, S], BF16)
    mp1 = arena.mark()
    KT, b_KT = arena.alloc("KT", [128, 4, S], BF16)
    V, b_V = arena.alloc("V", [128, NB, 512], BF16)
    qg_bc, b_qg = arena.alloc("qg_bc", [128, 128], F32)
    kg_bc, b_kg = arena.alloc("kg_bc", [128, 128], F32)
    P.emit("sp", lambda e: e.dma_start(out=qg_bc, in_=qnorm_g.partition_broadcast(128)), writes=[b_qg], dma_key="const")
    P.emit("sp", lambda e: e.dma_start(out=kg_bc, in_=knorm_g.partition_broadcast(128)), writes=[b_kg], dma_key="const")
    P.emit("dve", lambda e: e.tensor_scalar(out=qg_bc, in0=qg_bc, scalar1=128.0 ** -0.5, scalar2=None, op0=ALU.mult), reads=[b_qg], writes=[b_qg])
    load_weight_bf16(W1, b_W1, w_p1, 2048)

    QT, b_QT = arena.alloc("QT", [128, 4, 512], BF16)
    nrm, b_nrm = arena.alloc("nrm", [128, 4, 128], BF16)
    szb = [arena.alloc(f"szb{i}", [128, 512], F32) for i in range(4)]
    Et = [arena.alloc(f"E{i}", [128, 512], F32) for i in range(2)]
    SPt = [arena.alloc(f"SP{i}", [128, 512], BF16) for i in range(2)]
    Tt = [arena.alloc(f"T{i}", [128, 512], F32) for i in range(2)]
    Wt = [arena.alloc(f"Wt{i}", [128, 512], BF16) for i in range(2)]
    Rn = [arena.alloc(f"Rn{i}", [128, 512], F32) for i in range(2)]
    tmp3 = tmpf.rearrange("p (h d) -> p h d", h=4)

    for g in range(DBG["nt"]):
        compute_hdnT(g)
        for blk in range(4):
            tb = 4 * g + blk
            for ci, name in enumerate(("q", "k", "v")):
                pt, pb = proj_tok(W1, b_W1, ci * 512, blk)
                if name == "v":
                    P.emit("act", lambda e, pt=pt, tb=tb: e.activation(out=V[:, tb, :], in_=pt[:, :], func=AF.Copy), reads=[pb], writes=[b_V])
                    continue
                g_t, g_b = (qg_bc, b_qg) if name == "q" else (kg_bc, b_kg)
                headnorm_psum(pt, pb, tmp3, b_tmpf)
                P.emit("pool", lambda e, g_t=g_t: e.tensor_tensor(out=nrm, in0=tmp3, in1=g_t.unsqueeze(1).to_broadcast([128, 4, 128]), op=ALU.mult),
                       reads=[b_tmpf, g_b], writes=[b_nrm])
                tpt, tpb = PS[3]
                tpv = tpt[:].bitcast(BF16)
                for h in range(4):
                    P.emit("pe", lambda e, h=h, tpv=tpv: e.transpose(tpv[:, h * 128:(h + 1) * 128], nrm[:, h, :], ident[:]),
                           reads=[b_nrm, b_ident], writes=[tpb])
                src = tpv[:, 0:512].rearrange("p (h t) -> p h t", h=4)
                if name == "q":
                    P.emit("dve", lambda e, src=src, blk=blk: e.tensor_copy(out=QT[:, :, blk * 128:(blk + 1) * 128], in_=src), reads=[tpb], writes=[b_QT])
                else:
                    P.emit("dve", lambda e, src=src, tb=tb: e.tensor_copy(out=KT[:, :, tb * 128:(tb + 1) * 128], in_=src), reads=[tpb], writes=[b_KT])
        for cc in range(4):
            pt, pb = proj_feat(W1, b_W1, 1536 + cc * 128)
            zt, zb = szb[cc]
            P.emit("act", lambda e, pt=pt, zt=zt: e.activation(out=zt, in_=pt[:, :], func=AF.Silu), reads=[pb], writes=[zb])

        steps = [(h, kb) for h in range(4) for kb in range(4 * g + 3, -1, -1)]
        n = len(steps)
        Zb = [PS[4], PS[5]]
        Pp, S2p = PS[6], PS[7]
        Ob = [PS[2], PS[3]]

        def stageA(i, g=g):
            h, kb = steps[i]
            c0 = max(0, kb - 4 * g) * 128
            zt, zb = Zb[i % 2]
            et, eb = Et[i % 2]
            spt, spb = SPt[i % 2]
            P.emit("pe", lambda e: e.matmul(zt[:, c0:512], lhsT=KT[:, h, kb * 128:(kb + 1) * 128], rhs=QT[:, h, c0:512], start=True, stop=True),
                   reads=[b_KT, b_QT], writes=[zb])
            P.emit("act", lambda e: e.activation(out=et[:, c0:512], in_=zt[:, c0:512], func=AF.Exp), reads=[zb], writes=[eb])
            P.emit("act", lambda e: e.activation(out=spt[:, c0:512], in_=et[:, c0:512], func=AF.Ln, bias=1.0), reads=[eb], writes=[spb])
            if kb >= 4 * g:
                P.emit("pool", lambda e: e.tensor_tensor(out=spt[:, c0:c0 + 128], in0=spt[:, c0:c0 + 128], in1=mask01[:], op=ALU.mult),
                       reads=[b_mask01], writes=[spb])

        def stageB1(i, g=g):
            h, kb = steps[i]
            c0 = max(0, kb - 4 * g) * 128
            spt, spb = SPt[i % 2]
            tt, tb_ = Tt[i % 2]
            wt, wb = Wt[i % 2]
            rt, rb = Rn[h % 2]
            pt, pb = Pp
            st_, sbb = S2p
            if kb == 4 * g + 3:
                P.emit("pool", lambda e: e.memset(rt, 0.0), writes=[rb])
            P.emit("pe", lambda e: e.matmul(pt[:, c0:512], lhsT=KT[:, h, kb * 128:(kb + 1) * 128], rhs=QT[:, h, c0:512], start=True, stop=False),
                   reads=[b_KT, b_QT], writes=[pb])
            P.emit("pe", lambda e: e.matmul(pt[:, c0:512], lhsT=negtri[:], rhs=spt[:, c0:512], start=False, stop=True),
                   reads=[spb, b_negtri], writes=[pb])
            P.emit("pe", lambda e: e.matmul(st_[:, c0:512], lhsT=negones[:], rhs=spt[:, c0:512], start=True, stop=True),
                   reads=[spb, b_negones], writes=[sbb])
            P.emit("dve", lambda e: e.tensor_tensor(out=tt[:, c0:512], in0=pt[:, c0:512], in1=rt[:, c0:512], op=ALU.add),
                   reads=[pb, rb], writes=[tb_])
            P.emit("dve", lambda e: e.tensor_tensor(out=rt[:, c0:512], in0=st_[:, c0:512], in1=rt[:, c0:512], op=ALU.add),
                   reads=[sbb], writes=[rb])
            P.emit("act", lambda e: e.activation(out=wt[:, c0:512], in_=tt[:, c0:512], func=AF.Exp), reads=[tb_], writes=[wb])
            if kb >= 4 * g:
                P.emit("pool", lambda e: e.tensor_tensor(out=wt[:, c0:c0 + 128], in0=wt[:, c0:c0 + 128], in1=mask01[:], op=ALU.mult),
                       reads=[b_mask01], writes=[wb])

        def stageB2(i, g=g):
            h, kb = steps[i]
            c0 = max(0, kb - 4 * g) * 128
            wt, wb = Wt[i % 2]
            ot, ob = Ob[h % 2]
            if kb == 4 * g + 3:
                P.emit("pe", lambda e: e.matmul(ot[:, :], lhsT=zerosb[:], rhs=QT[:, h, :], start=True, stop=False),
                       reads=[b_zerosb, b_QT], writes=[ob])
            P.emit("pe", lambda e: e.matmul(ot[:, c0:512], lhsT=V[:, kb, h * 128:(h + 1) * 128], rhs=wt[:, c0:512], start=False, stop=(kb == 0)),
                   reads=[b_V, wb], writes=[ob])
            if kb == 0:
                zt, zb = szb[h]
                yb_out = YB[:, h, g * 512:(g + 1) * 512]
                P.emit("dve", lambda e: e.tensor_tensor(out=yb_out, in0=ot[:, :], in1=zt, op=ALU.mult), reads=[ob, zb], writes=[b_YB])

        for i in range(n + 2):
            if i < n:
                stageA(i)
            if 1 <= i <= n:
                stageB1(i - 1)
            if i >= 2:
                stageB2(i - 2)

    arena.release(mp1, P)
    Wo, b_Wo = arena.alloc("Wo", [128, 8, D], BF16)
    WsT, b_WsT = arena.alloc("WsT", [128, 4, 128], BF16)
    wst_f, b_wstf = arena.alloc("wst_f", [128, 4, 128], F32)
    wst_b, b_wstb = arena.alloc("wst_b", [128, 4, 128], BF16)
    vgn, b_vgn = arena.alloc("vgn", [128, 4, 512], BF16)
    usz, b_usz = arena.alloc("usz", [128, 512], F32)
    YA, b_YA = arena.alloc("YA", [128, 4, 512], BF16)
    vg_bc, b_vg = arena.alloc("vg_bc", [128, 512], F32)
    bs_row, b_bs = arena.alloc("bs_row", [128, 512], F32)
    xo = [arena.alloc(f"xo{i}", [128, D], F32) for i in range(2)]
    P.emit("sp", lambda e: e.dma_start(out=vg_bc, in_=vnorm_g.partition_broadcast(128)), writes=[b_vg], dma_key="const")
    P.emit("sp", lambda e: e.dma_start(out=bs_row[0:1, :], in_=a_bs[:, :]), writes=[b_bs], dma_key="const")
    load_weight_bf16(W1[:, :, 0:1536], b_W1, w_p2, 1536)
    load_weight_bf16(Wo, b_Wo, w_out, D)
    P.emit("sp", lambda e: e.dma_start(out=wst_f, in_=a_ws.rearrange("g t s -> t g s")), writes=[b_wstf], dma_key="const")
    P.emit("dve", lambda e: e.tensor_tensor(out=wst_b, in0=wst_f, in1=tril01[:].unsqueeze(1).to_broadcast([128, 4, 128]), op=ALU.mult),
           reads=[b_wstf, b_tril01], writes=[b_wstb])
    tpt, tpb = PS[3]
    tpv = tpt[:].bitcast(BF16)
    for gi in range(4):
        P.emit("pe", lambda e, gi=gi: e.transpose(tpv[:, gi * 128:(gi + 1) * 128], wst_b[:, gi, :], ident[:]), reads=[b_wstb, b_ident], writes=[tpb])
    P.emit("dve", lambda e: e.tensor_copy(out=WsT, in_=tpv[:, 0:512].rearrange("p (g t) -> p g t", g=4)), reads=[tpb], writes=[b_WsT])

    for g in range(DBG["nt"]):
        compute_hdnT(g)
        for blk in range(4):
            pt, pb = proj_tok(W1, b_W1, 0, blk)
            headnorm_psum(pt, pb, tmp3, b_tmpf)
            P.emit("pool", lambda e, blk=blk: e.tensor_tensor(out=vgn[:, blk, :], in0=tmpf, in1=vg_bc, op=ALU.mult), reads=[b_tmpf, b_vg], writes=[b_vgn])
        for gi in range(4):
            pz, pzb = proj_feat(W1, b_W1, 1024 + gi * 128)
            P.emit("act", lambda e, pz=pz: e.activation(out=tmpf, in_=pz[:, :], func=AF.Silu), reads=[pzb], writes=[b_tmpf])
            pu, pub = proj_feat(W1, b_W1, 512 + gi * 128)
            P.emit("dve", lambda e, pu=pu: e.tensor_tensor(out=usz, in0=pu[:, :], in1=tmpf, op=ALU.mult), reads=[pub, b_tmpf], writes=[b_usz])
            pm, pmb = PS[4 + (gi % 2)]
            for blk in range(4):
                P.emit("pe", lambda e, pm=pm, blk=blk, gi=gi: e.matmul(pm[:, blk * 128:(blk + 1) * 128], lhsT=vgn[:, blk, gi * 128:(gi + 1) * 128], rhs=WsT[:, gi, :],
                                                                       start=True, stop=False), reads=[b_vgn, b_WsT], writes=[pmb])
                P.emit("pe", lambda e, pm=pm, blk=blk, gi=gi: e.matmul(pm[:, blk * 128:(blk + 1) * 128], lhsT=ones_row[0:1, :], rhs=bs_row[0:1, gi * 128:(gi + 1) * 128],
                                                                       start=False, stop=True), reads=[b_bs, b_ones_row], writes=[pmb])
            P.emit("dve", lambda e, pm=pm, gi=gi: e.tensor_tensor(out=YA[:, gi, :], in0=pm[:, :], in1=usz, op=ALU.mult), reads=[pmb, b_usz], writes=[b_YA])

        def lhs_fn(ic, blk, g=g):
            if ic < 4:
                return YA[:, ic, blk * 128:(blk + 1) * 128], b_YA
            tb = 4 * g + blk
            return YB[:, ic - 4, tb * 128:(tb + 1) * 128], b_YB
        out_proj(g, Wo, b_Wo, lhs_fn, xo)


def build_layer1(C):
    import math
    nc, P, arena, PS, DR, L = C["nc"], C["P"], C["arena"], C["PS"], C["DR"], C["L"]
    ident, b_ident = C["ident"]
    maskle, b_maskle = C["maskle"]
    stat, b_stat, sq, b_sq, tmpf, b_tmpf = C["stat"], C["b_stat"], C["sq"], C["b_sq"], C["tmpf"], C["b_tmpf"]
    hdnT, b_hdnT = C["hdnT"], C["b_hdnT"]
    load_weight_bf16, compute_hdnT, headnorm_psum, acc_bank = C["load_weight_bf16"], C["compute_hdnT"], C["headnorm_psum"], C["acc_bank"]
    proj_tok, proj_feat, out_proj, rms_rstd, col_load = C["proj_tok"], C["proj_feat"], C["out_proj"], C["rms_rstd"], C["col_load"]
    w_p, w_out, cw_d, cscale_d = DR[(L, "w_p")], DR[(L, "w_out")], DR[(L, "cw")], DR[(L, "cscale")]
    qnorm_g, knorm_g, pos_d, lgam_d, invf_d = DR[(L, "qnorm_g")], DR[(L, "knorm_g")], DR[(L, "pos")], DR[(L, "lgam")], DR[(L, "invf")]
    TWO_PI = 2.0 * math.pi

    W, b_W = arena.alloc("W", [128, 8, 3584], BF16)
    Wo, b_Wo = arena.alloc("Wo", [128, 8, D], BF16)
    cwb, b_cwb = arena.alloc("cwb", [128, 4, 2, 128], BF16)
    COS, b_COS = arena.alloc("COS", [128, NB, 64], F32)
    SIN, b_SIN = arena.alloc("SIN", [128, NB, 64], F32)
    sm, b_sm = arena.alloc("sm", [128, 64], F32)
    CS0, LG0, NL0, DQ0, DK0, CD0, P10 = 0, 4, 8, 12, 16, 20, 24
    qg_bc, b_qg = arena.alloc("qg_bc", [128, 128], F32)
    kg_bc, b_kg = arena.alloc("kg_bc", [128, 128], F32)
    St, b_St = arena.alloc("St", [128, 4, 128], F32)
    Sb, b_Sb = arena.alloc("Sb", [128, 4, 128], BF16)
    halo, b_halo = arena.alloc("halo", [128, 8, 16], F32)
    invc, b_invc = arena.alloc("invc", [128, 4, 16], F32)

    P.emit("sp", lambda e: e.dma_start(out=qg_bc, in_=qnorm_g.partition_broadcast(128)), writes=[b_qg], dma_key="const")
    P.emit("sp", lambda e: e.dma_start(out=kg_bc, in_=knorm_g.partition_broadcast(128)), writes=[b_kg], dma_key="const")
    P.emit("sp", lambda e: e.dma_start(out=sm[:, LG0:LG0 + 4], in_=lgam_d.partition_broadcast(128)), writes=[b_sm], dma_key="const")
    col_load(sm[:, CS0:CS0 + 4], cscale_d, b_sm)
    P.emit("pool", lambda e: e.memset(St, 0.0), writes=[b_St])
    P.emit("pool", lambda e: e.memset(Sb, 0.0), writes=[b_Sb])
    P.emit("pool", lambda e: e.memset(halo, 0.0), writes=[b_halo])

    ms = arena.mark()
    posi, b_posi = arena.alloc("posi", [128, NB], I32)
    posf, b_posf = arena.alloc("posf", [128, NB], F32)
    invf_bc, b_invf = arena.alloc("invf_bc", [128, 64], F32)
    p1i, b_p1i = arena.alloc("p1i", [128, 16], I32)
    t16, b_t16 = arena.alloc("t16", [128, 16], F32)
    cwf, b_cwf = arena.alloc("cwf", [128, 4, 2, 128], F32)
    ang, b_ang = arena.alloc("ang", [128, 8, 64], F32)
    ru, b_ru = arena.alloc("ru", [128, 8, 64], F32)
    rk, b_rk = arena.alloc("rk", [128, 8, 64], I32)
    rr, b_rr = arena.alloc("rr", [128, 8, 64], F32)
    rc, b_rc = arena.alloc("rc", [128, 8, 64], F32)

    P.emit("sp", lambda e: e.dma_start(out=posi.unsqueeze(2), in_=pos_d.rearrange("o (k p u) -> p (o k) u", p=128, u=1), allow_slow_non_contiguous=True),
           writes=[b_posi], dma_key="const")
    P.emit("sp", lambda e: e.dma_start(out=invf_bc, in_=invf_d.partition_broadcast(128)), writes=[b_invf], dma_key="const")
    P.emit("sp", lambda e: e.dma_start(out=cwf, in_=cw_d.rearrange("g (cc p) e -> p g cc e", p=128)), writes=[b_cwf], dma_key="const")
    P.emit("dve", lambda e: e.tensor_copy(out=cwb, in_=cwf), reads=[b_cwf], writes=[b_cwb])
    P.emit("dve", lambda e: e.tensor_copy(out=posf, in_=posi), reads=[b_posi], writes=[b_posf])
    P.emit("pool", lambda e: e.iota(p1i[:, 0:1], pattern=[[0, 1]], base=1, channel_multiplier=1), writes=[b_p1i])
    P.emit("dve", lambda e: e.tensor_copy(out=sm[:, P10:P10 + 1], in_=p1i[:, 0:1]), reads=[b_p1i], writes=[b_sm])
    P.emit("dve", lambda e: e.tensor_scalar(out=sm[:, NL0:NL0 + 4], in0=sm[:, LG0:LG0 + 4], scalar1=-1.0, scalar2=None, op0=ALU.mult), reads=[b_sm], writes=[b_sm])
    for h in range(4):
        P.emit("act", lambda e, h=h: e.activation(out=sm[:, DQ0 + h:DQ0 + h + 1], in_=sm[:, P10:P10 + 1], func=AF.Exp, scale=sm[:, LG0 + h:LG0 + h + 1]),
               reads=[b_sm], writes=[b_sm])
        P.emit("act", lambda e, h=h: e.activation(out=sm[:, DK0 + h:DK0 + h + 1], in_=sm[:, P10:P10 + 1], func=AF.Exp, scale=sm[:, NL0 + h:NL0 + h + 1]),
               reads=[b_sm], writes=[b_sm])
    P.emit("act", lambda e: e.activation(out=sm[:, CD0:CD0 + 4], in_=sm[:, LG0:LG0 + 4], func=AF.Exp, scale=128.0), reads=[b_sm], writes=[b_sm])
    P.emit("dve", lambda e: e.tensor_scalar(out=sm[:, DK0:DK0 + 4], in0=sm[:, DK0:DK0 + 4], scalar1=128.0 ** -0.5, scalar2=None, op0=ALU.mult), reads=[b_sm], writes=[b_sm])
    P.emit("pool", lambda e: e.iota(p1i[:, 0:16], pattern=[[1, 16]], base=1, channel_multiplier=0), reads=[b_sm], writes=[b_p1i])
    P.emit("dve", lambda e: e.tensor_copy(out=t16, in_=p1i[:, 0:16]), reads=[b_p1i], writes=[b_t16])
    for gi, win in enumerate((2, 4, 8, 16)):
        P.emit("dve", lambda e, gi=gi, win=win: e.tensor_scalar(out=invc[:, gi, :], in0=t16, scalar1=float(win), scalar2=None, op0=ALU.min), reads=[b_t16], writes=[b_invc])
    P.emit("dve", lambda e: e.reciprocal(out=invc, in_=invc), reads=[b_invc], writes=[b_invc])
    A2 = lambda t: t.rearrange("p k j -> p (k j)")
    for sl in range(4):
        ks = slice(sl * 8, (sl + 1) * 8)
        P.emit("dve", lambda e, ks=ks: e.tensor_tensor(out=ang, in0=posf[:, ks].unsqueeze(2).to_broadcast([128, 8, 64]),
                                                       in1=invf_bc.unsqueeze(1).to_broadcast([128, 8, 64]), op=ALU.mult),
               reads=[b_posf, b_invf], writes=[b_ang])
        P.emit("dve", lambda e: e.tensor_scalar(out=ru, in0=ang, scalar1=1.0 / TWO_PI, scalar2=None, op0=ALU.mult), reads=[b_ang], writes=[b_ru])
        P.emit("dve", lambda e: e.tensor_copy(out=rk, in_=ru), reads=[b_ru], writes=[b_rk])
        P.emit("dve", lambda e: e.tensor_copy(out=ru, in_=rk), reads=[b_rk], writes=[b_ru])
        P.emit("dve", lambda e: e.scalar_tensor_tensor(out=rr, in0=ru, scalar=-TWO_PI, in1=ang, op0=ALU.mult, op1=ALU.add), reads=[b_ru, b_ang], writes=[b_rr])

        def wrap(t, b_t):
            P.emit("dve", lambda e: e.tensor_scalar(out=ru, in0=t, scalar1=math.pi, scalar2=None, op0=ALU.is_gt), reads=[b_t], writes=[b_ru])
            P.emit("dve", lambda e: e.scalar_tensor_tensor(out=t, in0=ru, scalar=-TWO_PI, in1=t, op0=ALU.mult, op1=ALU.add), reads=[b_ru], writes=[b_t])
            P.emit("dve", lambda e: e.tensor_scalar(out=ru, in0=t, scalar1=-math.pi, scalar2=None, op0=ALU.is_lt), reads=[b_t], writes=[b_ru])
            P.emit("dve", lambda e: e.scalar_tensor_tensor(out=t, in0=ru, scalar=TWO_PI, in1=t, op0=ALU.mult, op1=ALU.add), reads=[b_ru], writes=[b_t])
            P.emit("dve", lambda e: e.tensor_scalar(out=t, in0=t, scalar1=math.pi, scalar2=-math.pi, op0=ALU.min, op1=ALU.max), reads=[b_t], writes=[b_t])
        wrap(rr, b_rr)
        P.emit("dve", lambda e: e.tensor_scalar(out=rc, in0=rr, scalar1=math.pi / 2, scalar2=None, op0=ALU.add), reads=[b_rr], writes=[b_rc])
        wrap(rc, b_rc)
        P.emit("act", lambda e, ks=ks: e.activation(out=SIN[:, ks, :], in_=rr, func=AF.Sin), reads=[b_rr], writes=[b_SIN])
        P.emit("act", lambda e, ks=ks: e.activation(out=COS[:, ks, :], in_=rc, func=AF.Sin), reads=[b_rc], writes=[b_COS])
    arena.release(ms, P)

    load_weight_bf16(W, b_W, w_p, 3584)
    load_weight_bf16(Wo, b_Wo, w_out, D)

    rq, b_rq = arena.alloc("rq", [128, 4, 128], F32)
    ro, b_ro = arena.alloc("ro", [128, 4, 128], F32)
    ra, b_ra = arena.alloc("ra", [128, 4, 64], F32)
    rb_, b_rb = arena.alloc("rb", [128, 4, 64], F32)
    rdb, b_rdb = arena.alloc("rdb", [128, 4, 128], BF16)
    QTt, b_QTt = arena.alloc("QTt", [128, 4, 512], BF16)
    KTt, b_KTt = arena.alloc("KTt", [128, 4, 512], BF16)
    KTOK, b_KTOK = arena.alloc("KTOK", [128, 4, 512], BF16)
    Vt, b_Vt = arena.alloc("Vt", [128, 4, 512], BF16)
    scb, b_scb = arena.alloc("scb", [128, 4, 128], BF16)
    onb, b_onb = arena.alloc("onb", [128, 4, 128], BF16)
    YC, b_YC = arena.alloc("YC", [128, 4, 512], BF16)
    YD, b_YD = arena.alloc("YD", [128, 4, 512], BF16)
    pcw, b_pcw = arena.alloc("pcw", [128, 528], F32)
    wa, b_wa = arena.alloc("wa", [128, 528], F32)
    wb_, b_wb = arena.alloc("wb", [128, 528], F32)
    pl, b_pl = arena.alloc("pl", [128, 2, 512], BF16)
    xo = [arena.alloc(f"xo{i}", [128, D], F32) for i in range(2)]
    tmp3 = tmpf.rearrange("p (h d) -> p h d", h=4)

    def bc_h(ap64):
        return ap64.unsqueeze(1).to_broadcast([128, 4, 64])

    stage = DBG["stage"]
    if stage < 9:
        P.emit("pool", lambda e: e.memset(YC, 0.0), writes=[b_YC])
        P.emit("pool", lambda e: e.memset(YD, 0.0), writes=[b_YD])
    for g in range(DBG["nt"]):
        compute_hdnT(g)
        for blk in (range(4) if stage >= 1 else []):
            tb = 4 * g + blk
            for ci, name in enumerate(("q", "k", "v")):
                pt, pb = proj_tok(W, b_W, ci * 512, blk)
                if name == "v":
                    P.emit("act", lambda e, pt=pt, blk=blk: e.activation(out=Vt[:, blk, :], in_=pt[:, :], func=AF.Copy), reads=[pb], writes=[b_Vt])
                    continue
                g_t, g_b = (qg_bc, b_qg) if name == "q" else (kg_bc, b_kg)
                d0 = DQ0 if name == "q" else DK0
                headnorm_psum(pt, pb, tmp3, b_tmpf)
                P.emit("pool", lambda e, g_t=g_t: e.tensor_tensor(out=rq, in0=tmp3, in1=g_t.unsqueeze(1).to_broadcast([128, 4, 128]), op=ALU.mult),
                       reads=[b_tmpf, g_b], writes=[b_rq])
                cosb, sinb = bc_h(COS[:, tb, :]), bc_h(SIN[:, tb, :])
                t1, t2 = rq[:, :, 0:64], rq[:, :, 64:128]
                P.emit("dve", lambda e, cosb=cosb: e.tensor_tensor(out=ra, in0=t1, in1=cosb, op=ALU.mult), reads=[b_rq, b_COS], writes=[b_ra])
                P.emit("pool", lambda e, sinb=sinb: e.tensor_tensor(out=rb_, in0=t2, in1=sinb, op=ALU.mult), reads=[b_rq, b_SIN], writes=[b_rb])
                P.emit("dve", lambda e: e.tensor_tensor(out=ro[:, :, 0:64], in0=ra, in1=rb_, op=ALU.subtract), reads=[b_ra, b_rb], writes=[b_ro])
                P.emit("pool", lambda e, sinb=sinb: e.tensor_tensor(out=ra, in0=t1, in1=sinb, op=ALU.mult), reads=[b_rq, b_SIN], writes=[b_ra])
                P.emit("dve", lambda e, cosb=cosb: e.tensor_tensor(out=rb_, in0=t2, in1=cosb, op=ALU.mult), reads=[b_rq, b_COS], writes=[b_rb])
                P.emit("pool", lambda e: e.tensor_tensor(out=ro[:, :, 64:128], in0=ra, in1=rb_, op=ALU.add), reads=[b_ra, b_rb], writes=[b_ro])
                dst, b_dst = (rdb, b_rdb) if name == "q" else (KTOK[:, blk, :].rearrange("p (h d) -> p h d", h=4), b_KTOK)
                P.emit("dve", lambda e, dst=dst, d0=d0: e.tensor_tensor(out=dst, in0=ro, in1=sm[:, d0:d0 + 4].unsqueeze(2).to_broadcast([128, 4, 128]), op=ALU.mult),
                       reads=[b_ro, b_sm], writes=[b_dst])
                tpt, tpb = PS[3]
                tpv = tpt[:].bitcast(BF16)
                for h in range(4):
                    P.emit("pe", lambda e, h=h, tpv=tpv, dst=dst: e.transpose(tpv[:, h * 128:(h + 1) * 128], dst[:, h, :], ident[:]),
                           reads=[b_dst, b_ident], writes=[tpb])
                src = tpv[:, 0:512].rearrange("p (h t) -> p h t", h=4)
                TT, b_TT = (QTt, b_QTt) if name == "q" else (KTt, b_KTt)
                P.emit("act", lambda e, src=src, TT=TT, blk=blk: e.activation(out=TT[:, :, blk * 128:(blk + 1) * 128], in_=src, func=AF.Copy), reads=[tpb], writes=[b_TT])
        for blk in (range(4) if stage >= 2 else []):
            cs = slice(blk * 128, (blk + 1) * 128)
            psc, pscb = PS[4]
            for h in range(4):
                P.emit("pe", lambda e, h=h, cs=cs: e.matmul(psc[:, h * 128:(h + 1) * 128], lhsT=KTt[:, h, cs], rhs=QTt[:, h, cs], start=True, stop=True),
                       reads=[b_KTt, b_QTt], writes=[pscb])
            P.emit("dve", lambda e: e.tensor_tensor(out=scb, in0=psc[:, :].rearrange("p (h t) -> p h t", h=4),
                                                    in1=maskle[:].unsqueeze(1).to_broadcast([128, 4, 128]), op=ALU.mult),
                   reads=[pscb, b_maskle], writes=[b_scb])
            po, pob = PS[5]
            for h in range(4):
                P.emit("pe", lambda e, h=h, blk=blk: e.matmul(po[:, h * 128:(h + 1) * 128], lhsT=scb[:, h, :], rhs=Vt[:, blk, h * 128:(h + 1) * 128], start=True, stop=False),
                       reads=[b_scb, b_Vt], writes=[pob])
                P.emit("pe", lambda e, h=h, cs=cs: e.matmul(po[:, h * 128:(h + 1) * 128], lhsT=QTt[:, h, cs], rhs=Sb[:, h, :], start=False, stop=True),
                       reads=[b_QTt, b_Sb], writes=[pob])
            headnorm_psum(po, pob, onb, b_onb)
            tpt, tpb = PS[3]
            tpv = tpt[:].bitcast(BF16)
            for h in range(4):
                P.emit("pe", lambda e, h=h, tpv=tpv: e.transpose(tpv[:, h * 128:(h + 1) * 128], onb[:, h, :], ident[:]), reads=[b_onb, b_ident], writes=[tpb])
            P.emit("act", lambda e, tpv=tpv, cs=cs: e.activation(out=YD[:, :, cs], in_=tpv[:, 0:512].rearrange("p (h t) -> p h t", h=4), func=AF.Copy),
                   reads=[tpb], writes=[b_YD])
            pkv, pkvb = PS[6]
            for h in range(4):
                P.emit("pe", lambda e, h=h, blk=blk: e.matmul(pkv[:, h * 128:(h + 1) * 128], lhsT=KTOK[:, blk, h * 128:(h + 1) * 128], rhs=Vt[:, blk, h * 128:(h + 1) * 128],
                                                              start=True, stop=True), reads=[b_KTOK, b_Vt], writes=[pkvb])
            P.emit("dve", lambda e: e.tensor_tensor(out=St, in0=pkv[:, :].rearrange("p (h e) -> p h e", h=4), in1=St, op=ALU.add), reads=[pkvb], writes=[b_St])
            P.emit("pool", lambda e: e.tensor_tensor(out=St, in0=St, in1=sm[:, CD0:CD0 + 4].unsqueeze(2).to_broadcast([128, 4, 128]), op=ALU.mult),
                   reads=[b_sm], writes=[b_St])
            P.emit("pool", lambda e: e.tensor_copy(out=Sb, in_=St), reads=[b_St], writes=[b_Sb])
        for h in (range(4) if stage >= 3 else []):
            pz, pzb = proj_feat(W, b_W, 3072 + h * 128)
            P.emit("act", lambda e, pz=pz: e.activation(out=tmpf, in_=pz[:, :], func=AF.Silu), reads=[pzb], writes=[b_tmpf])
            P.emit("dve", lambda e, h=h: e.tensor_tensor(out=YD[:, h, :], in0=YD[:, h, :], in1=tmpf, op=ALU.mult), reads=[b_tmpf], writes=[b_YD])
        for gi, win in (enumerate((2, 4, 8, 16)) if stage >= 4 else []):
            for cc in range(2):
                ch = 2 * gi + cc
                pp, ppb = proj_feat(W, b_W, 1536 + ch * 128)
                P.emit("pool", lambda e, ch=ch: e.tensor_copy(out=pcw[:, 0:16], in_=halo[:, ch, :]), reads=[b_halo], writes=[b_pcw])
                P.emit("act", lambda e, pp=pp: e.activation(out=pcw[:, 16:528], in_=pp[:, :], func=AF.Copy), reads=[ppb], writes=[b_pcw])
                P.emit("pool", lambda e, ch=ch: e.tensor_copy(out=halo[:, ch, :], in_=pcw[:, 512:528]), reads=[b_pcw], writes=[b_halo])
                cur, b_cur = pcw, b_pcw
                sh = 1
                lo = 0
                bufs = [(wa, b_wa), (wb_, b_wb)]
                k = 0
                while sh < win:
                    nxt, b_nxt = bufs[k % 2]
                    k += 1
                    lo2 = lo + sh
                    eng = "dve" if k % 2 == 1 else "pool"
                    P.emit(eng, lambda e, cur=cur, nxt=nxt, lo2=lo2, sh=sh: e.tensor_tensor(out=nxt[:, lo2:528], in0=cur[:, lo2:528], in1=cur[:, lo2 - sh:528 - sh], op=ALU.add),
                           reads=[b_cur], writes=[b_nxt])
                    cur, b_cur = nxt, b_nxt
                    lo = lo2
                    sh *= 2
                P.emit("dve", lambda e, cur=cur, cc=cc, win=win: e.scalar_tensor_tensor(out=pl[:, cc, :], in0=cur[:, 16:528], scalar=1.0 / win, in1=pcw[:, 16:528],
                                                                                        op0=ALU.mult, op1=ALU.subtract), reads=[b_cur, b_pcw], writes=[b_pl])
                if g == 0:
                    other, b_other = bufs[k % 2]
                    P.emit("dve", lambda e, cur=cur, other=other, gi=gi: e.tensor_tensor(out=other[:, 0:16], in0=cur[:, 16:32], in1=invc[:, gi, :], op=ALU.mult),
                           reads=[b_cur, b_invc], writes=[b_other])
                    P.emit("dve", lambda e, other=other, cc=cc: e.tensor_tensor(out=pl[:, cc, 0:16], in0=other[:, 0:16], in1=pcw[:, 16:32], op=ALU.subtract),
                           reads=[b_other, b_pcw], writes=[b_pl])
            pz, pzb = proj_feat(W, b_W, 2560 + gi * 128)
            P.emit("act", lambda e, pz=pz: e.activation(out=tmpf, in_=pz[:, :], func=AF.Silu), reads=[pzb], writes=[b_tmpf])
            pm, pmb = PS[7]
            for cc in range(2):
                P.emit("pe", lambda e, gi=gi, cc=cc: e.matmul(pm[:, :], lhsT=cwb[:, gi, cc, :], rhs=pl[:, cc, :], start=(cc == 0), stop=(cc == 1)),
                       reads=[b_cwb, b_pl], writes=[pmb])
            P.emit("dve", lambda e, gi=gi: e.scalar_tensor_tensor(out=YC[:, gi, :], in0=pm[:, :], scalar=sm[:, CS0 + gi:CS0 + gi + 1], in1=tmpf,
                                                                  op0=ALU.mult, op1=ALU.mult), reads=[pmb, b_sm, b_tmpf], writes=[b_YC])

        def lhs_fn(ic, blk):
            cs = slice(blk * 128, (blk + 1) * 128)
            if ic < 4:
                return YC[:, ic, cs], b_YC
            return YD[:, ic - 4, cs], b_YD
        out_proj(g, Wo, b_Wo, lhs_fn, xo)


_CACHE = {}


def _prog(mode):
    if mode not in _CACHE:
        _CACHE[mode] = build_program(mode)
    return _CACHE[mode]


def _f(a):
    return np.ascontiguousarray(a)


def _l0_params(inp, b, hh):
    w_in = inp["even_w_in"][0]
    sl = slice(512 * hh, 512 * hh + 512)
    u, v, q, k, val, z = (w_in[:, 0:1024], w_in[:, 1024:2048], w_in[:, 2048:3072], w_in[:, 3072:4096], w_in[:, 4096:5120], w_in[:, 5120:7168])
    z_a, z_b = z[:, 0:1024], z[:, 1024:2048]
    w_out = inp["even_w_out"][0]
    d = {
        "c": inp["c"][b:b + 1], "norm_g": inp["even_norm_g"][0:1], "w_mod": inp["even_w_mod"][0], "b_mod": inp["even_b_mod"][0:1],
        "w_p1": np.concatenate([q[:, sl], k[:, sl], val[:, sl], z_b[:, sl]], axis=1),
        "w_p2": np.concatenate([v[:, sl], u[:, sl], z_a[:, sl]], axis=1),
        "w_out": np.concatenate([w_out[512 * hh:512 * hh + 512], w_out[1024 + 512 * hh:1024 + 512 * hh + 512]], axis=0),
        "vnorm_g": inp["even_a_vnorm_g"][0, 4 * hh:4 * hh + 4].reshape(1, 512),
        "a_ws": inp["even_a_ws"][0, 4 * hh:4 * hh + 4],
        "a_bs": inp["even_a_bs"][0, 4 * hh:4 * hh + 4].reshape(1, 512),
        "qnorm_g": inp["even_b_qnorm_g"][0:1], "knorm_g": inp["even_b_knorm_g"][0:1],
    }
    return {f"p0_{k_}": _f(v_) for k_, v_ in d.items()}


def _l1_params(inp, b, hh):
    w_in = inp["odd_w_in"][0]
    sl = slice(512 * hh, 512 * hh + 512)
    pc, q, k, val, z = (w_in[:, 0:1024], w_in[:, 1024:2048], w_in[:, 2048:3072], w_in[:, 3072:4096], w_in[:, 4096:6144])
    esel = np.concatenate([np.arange(g * 256 + hh * 128, g * 256 + hh * 128 + 128) for g in range(4)])
    z_c = z[:, esel]
    z_d = z[:, 1024 + 512 * hh:1024 + 512 * hh + 512]
    w_out = inp["odd_w_out"][0]
    hidx = np.arange(4 * hh, 4 * hh + 4, dtype=np.float32)
    lgam = np.log1p(-np.exp2(-5.0 - hidx)).astype(np.float32).reshape(1, 4)
    invf = (np.float32(10000.0) ** (-np.arange(64, dtype=np.float32) / np.float32(64))).astype(np.float32).reshape(1, 64)
    d = {
        "c": inp["c"][b:b + 1], "norm_g": inp["odd_norm_g"][0:1], "w_mod": inp["odd_w_mod"][0], "b_mod": inp["odd_b_mod"][0:1],
        "w_p": np.concatenate([q[:, sl], k[:, sl], val[:, sl], pc, z_c, z_d], axis=1),
        "w_out": np.concatenate([w_out[esel], w_out[1024 + 512 * hh:1024 + 512 * hh + 512]], axis=0),
        "cw": inp["odd_c_w"][0][:, :, hh * 128:hh * 128 + 128],
        "cscale": inp["odd_c_scale"][0][esel].reshape(1, 512),
        "qnorm_g": inp["odd_d_qnorm_g"][0:1], "knorm_g": inp["odd_d_knorm_g"][0:1],
        "pos": inp["positions"][b:b + 1].astype(np.int32),
        "lgam": lgam, "invf": invf,
    }
    return {f"p1_{k_}": _f(v_) for k_, v_ in d.items()}


def run_l0(inp):
    nc = _prog("l0")
    maps = []
    for c in range(8):
        m = _l0_params(inp, c // 2, c % 2)
        m["x"] = _f(inp["x"][c // 2])
        maps.append(m)
    res = run_bass_kernel_spmd(nc, maps, core_ids=list(range(8)))
    return [r["fout"] for r in res.results]


def run_l1(inp, f0):
    nc = _prog("l1")
    maps = []
    for c in range(8):
        m = _l1_params(inp, c // 2, c % 2)
        m["xa"] = f0[2 * (c // 2)]
        m["xb"] = f0[2 * (c // 2) + 1]
        maps.append(m)
    res = run_bass_kernel_spmd(nc, maps, core_ids=list(range(8)))
    return [r["fout"] for r in res.results]


def run_fin(fs):
    nc = _prog("fin")
    maps = []
    for c in range(8):
        b, hh = c // 2, c % 2
        rows = slice(2048 * hh, 2048 * hh + 2048)
        maps.append({"fa": _f(fs[2 * b][rows]), "fb": _f(fs[2 * b + 1][rows])})
    res = run_bass_kernel_spmd(nc, maps, core_ids=list(range(8)))
    out = np.empty((4, S, D), np.float32)
    for c in range(8):
        b, hh = c // 2, c % 2
        out[b, 2048 * hh:2048 * hh + 2048] = res.results[c]["out"]
    return out


def run_fused(inp):
    nc = _prog("fused")
    maps = []
    for c in range(8):
        m = _l0_params(inp, c // 2, c % 2)
        m.update(_l1_params(inp, c // 2, c % 2))
        m["x"] = _f(inp["x"][c // 2])
        maps.append(m)
    res = run_bass_kernel_spmd(nc, maps, core_ids=list(range(8)))
    out = np.empty((4, S, D), np.float32)
    for c in range(8):
        b, hh = c // 2, c % 2
        out[b, 2048 * hh:2048 * hh + 2048] = res.results[c]["out"]
    return out


FUSED = False


def kernel(**inputs):
    inp = {k: np.asarray(v) for k, v in inputs.items()}
    if FUSED:
        return run_fused(inp)
    f0 = run_l0(inp)
    f1 = run_l1(inp, f0)
    return run_fin(f1)
```
